# Optimizing a Trainium2 kernel written in Bass

```python
import jax, jax.numpy as jnp
from jax import lax
import numpy as np

D_MODEL = 4096
BATCH = 2
SEQ = 8192
DEPTH = 1
DEC_BATCH = 8
DEC_SEQ = 2048
PAST_LEN = 128

HG_HEADS = 32
HG_DK = D_MODEL // HG_HEADS
HG_DV = D_MODEL // HG_HEADS
D_A = HG_HEADS * HG_DV
CHUNK = 64
FN_GROUPS = 4
D_B = D_MODEL // 2
FN_CG = D_B // FN_GROUPS
N_IN = 5 * D_A + 2 * D_B + 2 * D_MODEL
EPS = 1e-6

kernel_name = "hgrn2_fnet_gated_parallel_encoder"


def _rmsnorm(x, g):
    xf = x.astype(jnp.float32)
    y = xf * lax.rsqrt(jnp.mean(xf * xf, axis=-1, keepdims=True) + EPS)
    return (y * g.astype(jnp.float32)).astype(x.dtype)


def _gla_chunkwise(q, k, v, log_f):
    B, L, H, DK = q.shape
    DV = v.shape[-1]
    n = L // CHUNK

    def to_chunks(t):
        return t.astype(jnp.float32).reshape(B, n, CHUNK, H, t.shape[-1]).transpose(1, 0, 3, 2, 4)

    qc, kc, vc, gc = to_chunks(q), to_chunks(k), to_chunks(v), to_chunks(log_f)
    tri = jnp.tril(jnp.ones((CHUNK, CHUNK), dtype=bool))[None, None, :, :, None]

    def step(S, inp):
        qi, ki, vi, gi = inp
        b = jnp.cumsum(gi, axis=2)
        diff = b[:, :, :, None, :] - b[:, :, None, :, :]
        decay = jnp.exp(jnp.where(tri, diff, -jnp.inf))
        scores = jnp.einsum('bhtk,bhsk,bhtsk->bhts', qi, ki, decay)
        o = (jnp.einsum('bhts,bhsv->bhtv', scores, vi)
             + jnp.einsum('bhtk,bhkv->bhtv', qi * jnp.exp(b), S))
        b_last = b[:, :, -1:, :]
        S_new = (jnp.exp(b_last[:, :, 0, :])[..., None] * S
                 + jnp.einsum('bhsk,bhsv->bhkv', ki * jnp.exp(b_last - b), vi))
        return S_new, o

    S0 = jnp.zeros((B, H, DK, DV), jnp.float32)
    _, oc = lax.scan(step, S0, (qc, kc, vc, gc))
    return oc.transpose(1, 0, 3, 2, 4).reshape(B, L, H, DV)


def _forget(zf, lb):
    zf = zf.astype(jnp.float32)
    f = lb + (1.0 - lb) * jax.nn.sigmoid(zf)
    return jnp.log(f), (1.0 - lb) * jax.nn.sigmoid(-zf)


def _layer(x, g_norm, w_in, lb_fwd, lb_bwd, g_head, w_a, w_b, w_o):
    B, L, _ = x.shape
    h = _rmsnorm(x, g_norm)
    z = h @ w_in
    splits = [D_A, 2 * D_A, 3 * D_A, 4 * D_A, 5 * D_A,
              5 * D_A + D_B, 5 * D_A + 2 * D_B, 5 * D_A + 2 * D_B + D_MODEL]
    q, zf_f, zf_b, i_v, gate_a, u_b, gate_b, m_a, m_b = jnp.split(z, splits, axis=-1)

    heads = lambda t: t.reshape(B, L, HG_HEADS, -1)
    qh = heads(jax.nn.silu(q.astype(jnp.float32)) * (HG_DK ** -0.5))
    vh = heads(i_v.astype(jnp.float32))
    logf_f, k_f = _forget(zf_f, lb_fwd)
    logf_b, k_b = _forget(zf_b, lb_bwd)
    o_f = _gla_chunkwise(qh, heads(k_f), vh, heads(logf_f))
    rev = lambda t: jnp.flip(t, axis=1)
    o_b = rev(_gla_chunkwise(rev(qh), rev(heads(k_b)), rev(vh), rev(heads(logf_b))))
    o = o_f + o_b
    o = o * lax.rsqrt(jnp.mean(o * o, axis=-1, keepdims=True) + EPS) * g_head.astype(jnp.float32)
    a_out = o.reshape(B, L, D_A).astype(x.dtype) * jax.nn.silu(gate_a)
    p_a = a_out @ w_a

    u = u_b.astype(jnp.float32).reshape(B, L, FN_GROUPS, FN_CG)
    fu = jnp.fft.fft2(u, axes=(1, 3), norm="ortho").real.reshape(B, L, D_B)
    b_out = fu.astype(x.dtype) * jax.nn.silu(gate_b)
    p_b = b_out @ w_b

    m = jax.nn.sigmoid(m_a) * p_a + jax.nn.sigmoid(m_b) * p_b
    return x + m @ w_o


def _trunk(x, norm_gain, w_in, lbs_fwd, lbs_bwd, head_norm_gain, w_branch_a, w_branch_b, w_out, final_norm_gain):
    for l in range(DEPTH):
        x = _layer(x, norm_gain[l], w_in[l], lbs_fwd[l], lbs_bwd[l], head_norm_gain[l],
                   w_branch_a[l], w_branch_b[l], w_out[l])
    return _rmsnorm(x, final_norm_gain)


def setup_inputs(seed: int = 0) -> dict:
    key = jax.random.key(seed)
    ks = jax.random.split(key, 12)
    f32 = jnp.float32
    return {
        "x_prompt": jax.random.normal(ks[0], (BATCH, SEQ, D_MODEL), f32),
        "x_sample": jax.random.normal(ks[1], (DEC_BATCH, DEC_SEQ, D_MODEL), f32),
        "norm_gain": 1.0 + 0.02 * jax.random.normal(ks[2], (DEPTH, D_MODEL), f32),
        "w_in": jax.random.normal(ks[3], (DEPTH, D_MODEL, N_IN), f32) * D_MODEL ** -0.5,
        "lower_bounds_fwd": 0.1 * jax.random.normal(ks[4], (DEPTH + 1, D_A), f32),
        "lower_bounds_bwd": 0.1 * jax.random.normal(ks[5], (DEPTH + 1, D_A), f32),
        "head_norm_gain": 1.0 + 0.02 * jax.random.normal(ks[6], (DEPTH, HG_HEADS, HG_DV), f32),
        "w_branch_a": jax.random.normal(ks[7], (DEPTH, D_A, D_MODEL), f32) * D_A ** -0.5,
        "w_branch_b": jax.random.normal(ks[8], (DEPTH, D_B, D_MODEL), f32) * D_B ** -0.5,
        "w_out": jax.random.normal(ks[9], (DEPTH, D_MODEL, D_MODEL), f32) * D_MODEL ** -0.5,
        "final_norm_gain": 1.0 + 0.02 * jax.random.normal(ks[10], (D_MODEL,), f32),
    }


def reference(x_prompt, x_sample, norm_gain, w_in, lower_bounds_fwd, lower_bounds_bwd,
              head_norm_gain, w_branch_a, w_branch_b, w_out, final_norm_gain):
    lbs_fwd = jnp.cumsum(jax.nn.softmax(lower_bounds_fwd.astype(jnp.float32), axis=0), axis=0)
    lbs_bwd = jnp.cumsum(jax.nn.softmax(lower_bounds_bwd.astype(jnp.float32), axis=0), axis=0)
    y_prompt = _trunk(x_prompt, norm_gain, w_in, lbs_fwd, lbs_bwd, head_norm_gain,
                      w_branch_a, w_branch_b, w_out, final_norm_gain)
    y_sample = _trunk(x_sample, norm_gain, w_in, lbs_fwd, lbs_bwd, head_norm_gain,
                      w_branch_a, w_branch_b, w_out, final_norm_gain)
    return (y_prompt, y_sample)
```

```python
import numpy as np
import ml_dtypes
from contextlib import ExitStack
import concourse.bass as bass
import concourse.mybir as mybir
from concourse.bass_utils import run_bass_kernel_spmd

F32 = mybir.dt.float32
BF16 = mybir.dt.bfloat16
AF = mybir.ActivationFunctionType
ALU = mybir.AluOpType
EPS = 1e-6
CH = 64


class Cfg:
    def __init__(self, D, T):
        self.D, self.T = D, T
        self.H = D // 128
        self.DB = D // 2
        self.CG = self.DB // 4
        self.KC = D // 128
        self.KB = self.DB // 128
        self.KG = self.CG // 128
        self.NIN = 8 * D
        self.NSUB = min(512, T)
        self.NS = T // self.NSUB
        self.NCH = T // CH
        self.NT = T // 128
        self.CPB = self.NSUB // CH


class Op:
    __slots__ = ("eng", "fn", "deps", "sig", "sem", "val", "dma", "idx")


class Res:
    __slots__ = ("w", "r")

    def __init__(self):
        self.w = {}
        self.r = {}


ENGS = ["pe", "act", "dve", "pool", "sp"]
EPOCH = 30000
NSLOT = 8


class Sched:
    def __init__(self):
        self.ops = {e: [] for e in ENGS}
        self.res = {}
        self.pending = {e: [] for e in ENGS}
        self.ndma = {e: 0 for e in ENGS}
        self.alldma = []

    def _res(self, key):
        r = self.res.get(key)
        if r is None:
            r = self.res[key] = Res()
        return r

    def add(self, eng, fn, rd=(), wr=(), dma=False):
        op = Op()
        op.eng, op.fn, op.dma, op.sig, op.sem, op.val = eng, fn, dma, False, None, 0
        deps = {}
        psr = [k for k in rd if isinstance(k, tuple) and k[0] == "ps"]
        if psr:
            rd = [k for k in rd if not (isinstance(k, tuple) and k[0] == "ps")]
            wr = list(wr) + psr

        def dep(d):
            if d is op:
                return
            deps[id(d)] = d

        for k in rd:
            r = self._res(k)
            for key, w in r.w.items():
                if isinstance(w, list):
                    for x in w:
                        dep(x)
                else:
                    dep(w)
        for k in wr:
            r = self._res(k)
            for grp in (r.w, r.r):
                for key, w in grp.items():
                    if isinstance(w, list):
                        for x in w:
                            dep(x)
                    elif w.eng != eng or dma:
                        dep(w)
        for d in self.pending[eng]:
            dep(d)
        self.pending[eng] = []
        lst = []
        for d in deps.values():
            if d.eng == eng and eng == "pe" and not d.dma and not dma:
                continue
            lst.append(d)
            d.sig = True
        op.deps = lst
        for k in rd:
            r = self._res(k)
            if dma:
                r.r.setdefault("dma", []).append(op)
            else:
                r.r[eng] = op
        for k in wr:
            r = self._res(k)
            r.r = {}
            r.w = {"dma": [op]} if dma else {eng: op}
        if dma:
            op.idx = self.ndma[eng]
            self.ndma[eng] += 1
            self.alldma.append(op)
        self.ops[eng].append(op)
        return op

    def barrier(self):
        last = [self.ops[e][-1] for e in ENGS if self.ops[e]]
        deps = last + list(self.alldma)
        self.alldma = []
        for e in ENGS:
            self.pending[e] = list(deps)

    def emit(self, nc, stack):
        sems = {}

        def sem(key):
            if key not in sems:
                sems[key] = stack.enter_context(nc.semaphore("s_%s_%s" % key))
            return sems[key]

        for e in ENGS:
            cnt = 0
            for op in self.ops[e]:
                if op.dma:
                    op.sem = ("d" + e, op.idx % NSLOT)
                    op.val = 16 * (op.idx // NSLOT + 1)
                elif op.sig:
                    op.sem = (e, cnt // EPOCH)
                    op.val = cnt % EPOCH + 1
                    cnt += 1
        for e in ENGS:
            for op in self.ops[e]:
                if op.sem is not None:
                    sem(op.sem)
        tail = {}
        for op in self.alldma_all():
            tail[op.sem] = max(tail.get(op.sem, 0), op.val)

        def run(e, eng):
            waited = {}
            for op in self.ops[e]:
                waits = {}
                for d in op.deps:
                    waits[d.sem] = max(waits.get(d.sem, 0), d.val)
                if op.dma and op.val > 16:
                    waits[op.sem] = max(waits.get(op.sem, 0), op.val - 16)
                for k, v in waits.items():
                    if waited.get(k, 0) >= v:
                        continue
                    eng.wait_ge(sem(k), v)
                    waited[k] = v
                ins = op.fn(eng)
                if op.dma:
                    ins.then_inc(sem(op.sem), 16)
                elif op.sig:
                    ins.then_inc(sem(op.sem), 1)
            if e == "sp":
                for k, v in tail.items():
                    if waited.get(k, 0) < v:
                        eng.wait_ge(sem(k), v)

        with nc.Block() as block:
            @block.tensor
            def _(eng):
                run("pe", eng)

            @block.scalar
            def _(eng):
                run("act", eng)

            @block.vector
            def _(eng):
                run("dve", eng)

            @block.gpsimd
            def _(eng):
                run("pool", eng)

            @block.sync
            def _(eng):
                run("sp", eng)

    def alldma_all(self):
        for e in ENGS:
            for op in self.ops[e]:
                if op.dma:
                    yield op


class Arena:
    def __init__(self, ap, nbytes):
        self.ap, self.nbytes, self.off = ap, nbytes, 0

    def alloc(self, nelem, dtype, parts=128):
        sz = nelem * (4 if dtype == F32 else 2)
        sz = (sz + 63) // 64 * 64
        assert self.off + sz <= self.nbytes, ("arena overflow", self.off, sz, self.nbytes)
        a = self.ap[0:parts, self.off // 4:(self.off + sz) // 4]
        self.off += sz
        if dtype == BF16:
            a = a.bitcast(BF16)
        return a[:, 0:nelem]

    def mark(self):
        return self.off

    def release(self, m):
        self.off = m


def build(cfg):
    D, T, H, DB, CG, KC, KB, KG = cfg.D, cfg.T, cfg.H, cfg.DB, cfg.CG, cfg.KC, cfg.KB, cfg.KG
    NSUB, NS, NCH, NT, CPB = cfg.NSUB, cfg.NS, cfg.NCH, cfg.NT, cfg.CPB
    NIN = cfg.NIN
    nc = bass.Bass("TRN2", target_bir_lowering=False)
    S = Sched()

    def din(name, shape, dt=F32):
        return nc.dram_tensor(name, list(shape), dt, kind="ExternalInput").ap()

    class Multi:
        def __init__(self, name, lead, inner, dt):
            self.lead = lead
            if len(lead) == 1:
                self.items = [nc.dram_tensor("%s_%d" % (name, i), list(inner), dt).ap() for i in range(lead[0])]
            else:
                self.items = [Multi("%s_%d" % (name, i), lead[1:], inner, dt) for i in range(lead[0])]

        def __getitem__(self, key):
            if not isinstance(key, tuple):
                return self.items[key]
            it = self.items[key[0]]
            rest = key[1:]
            if isinstance(it, Multi):
                return it[rest] if len(rest) > 1 else it[rest[0]]
            return it[rest] if len(rest) > 1 else it[rest[0]]

    def dscr(name, shape, dt=BF16, lead=0):
        if lead:
            return Multi(name, list(shape[:lead]), list(shape[lead:]), dt)
        return nc.dram_tensor(name, list(shape), dt).ap()

    xb = din("xb", [5, T, D])
    w_in = din("w_in", [D, NIN])
    w_ctx = din("w_ctx", [3, D, D])
    w_a = din("w_a", [D, D])
    w_b = din("w_b", [DB, D])
    w_o = din("w_o", [D, D])
    vecs = din("vecs", [128, 16 * KC])
    gfin = din("gfin", [128, D])
    gnb = din("gnb", [128, D])
    msk = din("msk", [128, 8])
    tcc = din("tcc", [2, CG, CG], BF16)
    tseq = din("tseq", [5, 2, T, T], BF16)
    cmask = din("cmask", [64, 2, CPB * 64], BF16)
    m01 = din("m01", [128, T])
    identd = din("identd", [128, 128], BF16)
    y = nc.dram_tensor("y", [2, T, D], F32, kind="ExternalOutput").ap()

    wibs = [dscr("wib0", [D, NIN // 2]), dscr("wib1", [D, NIN // 2])]
    wcb = dscr("wcb", [3, D, D], lead=1)
    wab = dscr("wab", [D, D])
    wbb = dscr("wbb", [DB, D])
    wob = dscr("wob", [D, D])
    s_qs = dscr("s_qs", [2, D, T], lead=1)
    s_g = dscr("s_g", [5, 2, D, T], F32, lead=2)
    s_k = dscr("s_k", [5, 2, D, T], lead=2)
    s_v = dscr("s_v", [5, D, T], lead=1)
    s_ga = dscr("s_ga", [2, D, T], lead=1)
    s_u = dscr("s_u", [5, DB, T], lead=1)
    s_gb = dscr("s_gb", [2, DB, T], lead=1)
    s_ma = dscr("s_ma", [2, D, T], lead=1)
    s_mb = dscr("s_mb", [2, D, T], lead=1)
    s_ao = dscr("s_ao", [2, D, T], lead=1)
    s_bo = dscr("s_bo", [2, DB, T], lead=1)
    s_uc = dscr("s_uc", [5, 2, T, DB], lead=2)
    s_m = dscr("s_m", [2, D, T], lead=1)

    stack = ExitStack()
    ARB = 206 * 1024
    arena_t = stack.enter_context(nc.sbuf_tensor("arena", [128, ARB // 4], F32))
    A = Arena(arena_t, ARB)
    psb = [stack.enter_context(nc.psum_tensor("psb%d" % i, [128, 1024], BF16)) for i in range(2)]
    ps = [None, None] + [stack.enter_context(nc.psum_tensor("ps%d" % i, [128, 512], F32)) for i in range(2, 8)]

    def dma(eng, out, in_, rd, wr):
        return S.add(eng, lambda e: e.dma_start(out=out, in_=in_), rd=rd, wr=wr, dma=True)

    def cast(dst, src, rows, key):
        step = max(1, (32 << 20) // (src.shape[-1] * 4))
        for r0 in range(0, rows, step):
            r1 = min(rows, r0 + step)
            dma("pool", dst[r0:r1, :], src[r0:r1, :], rd=[], wr=[key])

    cast(wibs[0], w_in[:, 0:NIN // 2], D, "wib")
    cast(wibs[1], w_in[:, NIN // 2:NIN], D, "wib")
    for s in range(3):
        cast(wcb[s], w_ctx[s], D, "wcb")
    cast(wab, w_a, D, "wab")
    cast(wbb, w_b, DB, "wbb")
    cast(wob, w_o, D, "wob")

    ident = A.alloc(128, BF16)
    vec = A.alloc(16 * KC, F32)
    mk = A.alloc(8, F32)
    cm = A.alloc(2 * CPB * 64, BF16, parts=64)
    ones = A.alloc(128, BF16)
    lbt = A.alloc(5 * 3 * KC, F32)
    dma("sp", ident, identd, [], ["ident"])
    dma("sp", vec, vecs, [], ["vec"])
    dma("sp", mk, msk, [], ["mk"])
    dma("sp", cm, cmask.rearrange("p a c -> p (a c)"), [], ["cm"])
    S.add("dve", lambda e: e.memset(ones, 1.0 / 128), wr=["ones"])
    epsc = A.alloc(2, F32)
    S.add("dve", lambda e: e.memset(epsc, EPS), wr=["epsc"])

    def vcol(j, kc):
        return vec[:, j * KC + kc:j * KC + kc + 1]
    for si in range(5):
        a0 = vec[:, (2 + 2 * si) * KC:(3 + 2 * si) * KC]
        a1 = vec[:, (3 + 2 * si) * KC:(4 + 2 * si) * KC]
        lb = lbt[:, (si * 3 + 0) * KC:(si * 3 + 1) * KC]
        om = lbt[:, (si * 3 + 1) * KC:(si * 3 + 2) * KC]
        nm = lbt[:, (si * 3 + 2) * KC:(si * 3 + 3) * KC]
        S.add("dve", lambda e, lb=lb, a0=a0, a1=a1: e.tensor_tensor(out=lb, in0=a0, in1=a1, op=ALU.subtract),
              rd=["vec"], wr=["lbt"])
        S.add("act", lambda e, lb=lb: e.activation(out=lb, in_=lb, func=AF.Sigmoid), rd=["lbt"], wr=["lbt"])
        S.add("dve", lambda e, lb=lb, om=om: e.tensor_scalar(out=om, in0=lb, scalar1=-1.0, scalar2=1.0,
                                                             op0=ALU.mult, op1=ALU.add), rd=["lbt"], wr=["lbt"])
        S.add("dve", lambda e, nm=nm, om=om: e.tensor_scalar(out=nm, in0=om, scalar1=-1.0, scalar2=None,
                                                             op0=ALU.mult), rd=["lbt"], wr=["lbt"])

    def lbcol(si, which, h):
        return lbt[:, (si * 3 + which) * KC + h:(si * 3 + which) * KC + h + 1]

    s_st = dscr("s_st", [2, 128, H * 128], F32)

    cnt = {"ps": 0, "ev": 0}

    def p1(blk, hT):
        m = A.mark()
        xts = [A.alloc(D, F32) for _ in range(2)]
        hb = A.alloc(D, BF16)
        junk = A.alloc(D, BF16)
        ss = A.alloc(2, F32)
        gnbt = A.alloc(D, F32)
        dma("sp", gnbt, gnb, [], ["gnbt"])
        hT3 = hT.rearrange("p (k t) -> p k t", t=T)
        for i in range(NT):
            xt = xts[i % 2]
            xk = ("xt", i % 2)
            dma("sp", xt, xb[blk, i * 128:(i + 1) * 128, :], [], [xk])
            import os
            p1s = int(os.environ.get("P1STOP", "100"))
            if p1s < 2:
                continue
            S.add("dve", lambda e: e.memset(ss, 0.0), wr=["ss"])
            S.add("act", lambda e, xt=xt: e.activation(out=junk, in_=xt, func=AF.Square, scale=float(D) ** -0.5, accum_out=ss[:, 0:1]),
                  rd=[xk, "ss"], wr=["junk", "ss"])
            S.add("act", lambda e: e.activation(out=ss[:, 1:2], in_=ss[:, 0:1], func=AF.Sqrt, bias=epsc[:, 0:1]),
                  rd=["ss", "epsc"], wr=["ss1"])
            S.add("dve", lambda e: e.reciprocal(out=ss[:, 1:2], in_=ss[:, 1:2]), rd=["ss1"], wr=["ss1"])
            S.add("dve", lambda e, xt=xt: e.scalar_tensor_tensor(out=hb, in0=xt, scalar=ss[:, 1:2], in1=gnbt,
                                                                 op0=ALU.mult, op1=ALU.mult),
                  rd=[xk, "ss1", "gnbt"], wr=["hb"])
            for kc in range(KC):
                bank, piece = (kc // 8) % 2, kc % 8
                pk = ("ps", bank)
                pv = psb[bank][:, piece * 128:(piece + 1) * 128]
                S.add("pe", lambda e, pv=pv, kc=kc: e.transpose(out=pv, in_=hb[:, kc * 128:(kc + 1) * 128],
                                                               identity=ident), rd=["hb", "ident"], wr=[pk])
                if piece == 7 or kc == KC - 1:
                    k0 = kc - piece
                    npc = piece + 1
                    dst = hT3[:, k0:k0 + npc, i * 128:(i + 1) * 128]
                    src = psb[bank][:, 0:npc * 128].rearrange("p (a c) -> p a c", c=128)
                    if (kc // 8) % 2 == 0:
                        S.add("act", lambda e, dst=dst, src=src: e.copy(out=dst, in_=src), rd=[pk], wr=[("hT", i)])
                    else:
                        S.add("dve", lambda e, dst=dst, src=src: e.tensor_copy(out=dst, in_=src), rd=[pk], wr=[("hT", i)])
        return m

    def p2(blk, hT, ctx_slot):
        full = ctx_slot is None
        wts = [A.alloc(KC * 256, BF16) for _ in range(2)]
        stb = [A.alloc(T, BF16) for _ in range(2)]
        stf = [A.alloc(T, F32) for _ in range(1)]
        sgf = [A.alloc(T, F32) for _ in range(1)]
        hkeys = [("hT", i) for i in range(NT)]
        jobs = []
        if full:
            for j in range(0, NIN // 128):
                c0 = j * 128
                if c0 < D: t, ix = "q", j
                elif c0 < 2 * D: t, ix = "zfF", j - KC
                elif c0 < 3 * D: t, ix = "zfB", j - 2 * KC
                elif c0 < 4 * D: t, ix = "v", j - 3 * KC
                elif c0 < 5 * D: t, ix = "ga", j - 4 * KC
                elif c0 < 5 * D + DB: t, ix = "u", j - 5 * KC
                elif c0 < 6 * D: t, ix = "gb", j - 5 * KC - KB
                elif c0 < 7 * D: t, ix = "ma", j - 6 * KC
                else: t, ix = "mb", j - 7 * KC
                jobs.append((wibs[c0 // (NIN // 2)], "wib", c0 % (NIN // 2), t, ix))
        else:
            for j in range(KC):
                jobs.append((wcb[ctx_slot], "wcb", j * 128, "zfC", j))
            for j in range(KC):
                jobs.append((wibs[0], "wib", 3 * D + j * 128, "v", j))
            for j in range(KB):
                jobs.append((wibs[1], "wib", 5 * D + j * 128 - NIN // 2, "u", j))
        nb = {"b": 0, "f": 0, "g": 0, "w": 0}
        bi = blk if blk < 2 else None
        for jj in range(0, len(jobs), 2):
            grp = jobs[jj:jj + 2]
            wsrc, wkey, c0 = grp[0][0], grp[0][1], grp[0][2]
            wi = nb["w"] % 2
            nb["w"] += 1
            wt = wts[wi]
            wt3 = wt.rearrange("p (k c) -> p k c", c=256)
            ncol = 128 * len(grp)
            dma("sp", wt3[:, :, 0:ncol], wsrc[:, c0:c0 + ncol].rearrange("(k p) c -> p k c", p=128),
                [wkey], [("wt", wi)])
            for gi, (_, _, _, typ, ix) in enumerate(grp):
                isz = typ.startswith("zf")
                if isz:
                    si = {"zfF": 0, "zfB": 1, "zfC": 2 + (ctx_slot or 0)}[typ]
                    dr = 1 if typ == "zfB" else 0
                    sgi = 0
                    fi = 0
                    sg, sgk = sgf[sgi], ("sgf", sgi)
                    gst, gk = stf[fi], ("stf", fi)
                bbi = nb["b"] % 2; nb["b"] += 1
                st, stk = stb[bbi], ("stb", bbi)
                for n in range(NS):
                    bank = 2 + cnt["ps"] % 4
                    cnt["ps"] += 1
                    pk = ("ps", bank)
                    pv = ps[bank][:, 0:NSUB]
                    for kc in range(KC):
                        S.add("pe", lambda e, pv=pv, wt3=wt3, kc=kc, gi=gi, n=n: e.matmul(
                            pv, lhsT=wt3[:, kc, gi * 128:(gi + 1) * 128],
                            rhs=hT[:, kc * T + n * NSUB:kc * T + (n + 1) * NSUB],
                            start=(kc == 0), stop=(kc == KC - 1)),
                            rd=[("wt", wi)] + hkeys[n * (NSUB // 128):(n + 1) * (NSUB // 128)], wr=[pk])
                    sl = slice(n * NSUB, (n + 1) * NSUB)
                    if typ in ("q", "ga", "gb"):
                        S.add("act", lambda e, pv=pv, o=st[:, sl]: e.activation(out=o, in_=pv, func=AF.Silu),
                              rd=[pk], wr=[stk])
                    elif typ in ("ma", "mb"):
                        S.add("act", lambda e, pv=pv, o=st[:, sl]: e.activation(out=o, in_=pv, func=AF.Sigmoid),
                              rd=[pk], wr=[stk])
                    elif typ in ("v", "u"):
                        S.add("dve", lambda e, pv=pv, o=st[:, sl]: e.tensor_copy(out=o, in_=pv), rd=[pk], wr=[stk])
                    else:
                        S.add("act", lambda e, pv=pv, o=sg[:, sl]: e.activation(out=o, in_=pv, func=AF.Sigmoid),
                              rd=[pk], wr=[sgk])
                rows = slice(ix * 128, (ix + 1) * 128)
                if isz:
                    S.add("act", lambda e, sg=sg, gst=gst, si=si, ix=ix: e.activation(
                        out=gst, in_=sg, func=AF.Ln, scale=lbcol(si, 1, ix), bias=lbcol(si, 0, ix)),
                        rd=[sgk, "lbt"], wr=[gk])
                    S.add("dve", lambda e, sg=sg, st=st, si=si, ix=ix: e.tensor_scalar(
                        out=st, in0=sg, scalar1=lbcol(si, 2, ix), scalar2=lbcol(si, 1, ix), op0=ALU.mult, op1=ALU.add),
                        rd=[sgk, "lbt"], wr=[stk])
                    dma("pool", s_g[blk, dr, rows, :], gst, [gk], [("s_g", blk, dr, ix)])
                    dma("pool", s_k[blk, dr, rows, :], st, [stk], [("s_k", blk, dr, ix)])
                else:
                    dst = {"q": s_qs, "v": s_v, "ga": s_ga, "u": s_u, "gb": s_gb, "ma": s_ma, "mb": s_mb}[typ]
                    bsel = blk
                    dma("pool", dst[bsel, rows, :], st, [stk], [("s_" + typ, blk, ix)])

    def p3(blk, ctx_slot):
        full = ctx_slot is None
        dirs = [0, 1] if full else [0]
        lm = A.mark()
        SF = A.alloc(H * 128, F32)
        SB = A.alloc(H * 128, F32)
        if ctx_slot == 0:
            S.add("dve", lambda e: e.memset(SF, 0.0), wr=["SF"])
            S.add("dve", lambda e: e.memset(SB, 0.0), wr=["SB"])
        elif blk != 0:
            dma("sp", SF, s_st[0], ["s_st"], ["SF"])
            dma("sp", SB, s_st[1], ["s_st"], ["SB"])
        names = (["qs", "ga"] if full else []) + ["v"]
        bufs = {}
        for nm in names:
            bufs[nm] = [A.alloc(T, BF16) for _ in range(2)]
        for d in dirs:
            bufs[("g", d)] = [A.alloc(T, F32) for _ in range(2)]
            bufs[("k", d)] = [A.alloc(T, BF16) for _ in range(2)]
        m01t = A.alloc(T, F32)
        dma("sp", m01t, m01, [], ["m01"])
        bb = A.alloc(T, F32)
        ee = A.alloc(T, F32)
        qt = [A.alloc(T, BF16) for _ in dirs]
        kt = [A.alloc(T, BF16) for _ in dirs]
        ktok = A.alloc(NCH * 128, BF16, parts=64)
        vtok = A.alloc(NCH * 128, BF16, parts=64)
        scT = [A.alloc(NCH * 64, BF16, parts=64) for _ in dirs]
        Xall = [A.alloc(NCH * 128, BF16) for _ in dirs]
        dcol = A.alloc(NCH + 1, F32)
        Tst = A.alloc(128, F32)
        osb = A.alloc(NSUB, F32)
        osq = A.alloc(NSUB, BF16)
        rst = A.alloc(NSUB, F32)
        ast = [A.alloc(T, BF16) for _ in range(2)]
        if not full:
            sel = A.alloc(128, F32)
            tmpS = A.alloc(128, F32)
        S.add("dve", lambda e: e.memset(dcol[:, NCH:NCH + 1], 1.0), wr=["dcol1"])

        def loads(h):
            p = h % 2
            rows = slice(h * 128, (h + 1) * 128)
            if full:
                dma("sp", bufs["qs"][p], s_qs[blk, rows, :], [("s_q", blk, h)], [("l_qs", p)])
                dma("sp", bufs["ga"][p], s_ga[blk, rows, :], [("s_ga", blk, h)], [("l_ga", p)])
            dma("sp", bufs["v"][p], s_v[blk, rows, :], [("s_v", blk, h)], [("l_v", p)])
            for d in dirs:
                dma("sp", bufs[("g", d)][p], s_g[blk, d, rows, :], [("s_g", blk, d, h)], [("l_g", d, p)])
                dma("sp", bufs[("k", d)][p], s_k[blk, d, rows, :], [("s_k", blk, d, h)], [("l_k", d, p)])

        loads(0)
        for h in range(H):
            if h + 1 < H:
                loads(h + 1)
            p = h % 2
            vT = bufs["v"][p]
            for c in range(NCH):
                bank, piece = (c // 8) % 2, c % 8
                pv = psb[bank][0:64, piece * 128:(piece + 1) * 128]
                S.add("pe", lambda e, pv=pv, c=c, vT=vT: e.transpose(out=pv, in_=vT[:, c * 64:(c + 1) * 64],
                                                                    identity=ident),
                      rd=[("l_v", p), "ident"], wr=[("ps", bank)])
                if piece == 7 or c == NCH - 1:
                    n0 = c - piece
                    S.add("act", lambda e, bank=bank, n0=n0, piece=piece: e.copy(
                        out=vtok[:, n0 * 128:(n0 + piece + 1) * 128], in_=psb[bank][0:64, 0:(piece + 1) * 128]),
                        rd=[("ps", bank)], wr=["vtok"])
            for di, d in enumerate(dirs):
                g = bufs[("g", d)][p]
                kk = bufs[("k", d)][p]
                bwd = (d == 1)
                S.add("dve", lambda e, g=g: e.tensor_tensor_scan(out=bb, data0=m01t, data1=g, initial=0.0,
                                                                 op0=ALU.mult, op1=ALU.add),
                      rd=[("l_g", d, p), "m01"], wr=["bb"])
                S.add("act", lambda e: e.activation(out=dcol[:, 0:NCH], in_=bb[:, CH - 1:T:CH], func=AF.Exp),
                      rd=["bb"], wr=["dcol"])
                if bwd:
                    S.add("dve", lambda e, g=g: e.tensor_tensor(out=bb, in0=bb, in1=g, op=ALU.subtract),
                          rd=["bb", ("l_g", d, p)], wr=["bb"])
                S.add("act", lambda e, bwd=bwd: e.activation(out=ee, in_=bb, func=AF.Exp,
                                                             scale=(1.0 if bwd else -1.0)),
                      rd=["bb"], wr=["ee"])
                S.add("dve", lambda e, kk=kk, di=di: e.tensor_tensor(out=kt[di], in0=kk, in1=ee, op=ALU.mult),
                      rd=["ee", ("l_k", d, p)], wr=[("kt", di)])
                if full:
                    S.add("act", lambda e, bwd=bwd: e.activation(out=ee, in_=bb, func=AF.Exp,
                                                                 scale=(-1.0 if bwd else 1.0)),
                          rd=["bb"], wr=["ee"])
                    qs = bufs["qs"][p]
                    S.add("dve", lambda e, qs=qs, di=di: e.scalar_tensor_tensor(
                        out=qt[di], in0=qs, scalar=128.0 ** -0.5, in1=ee, op0=ALU.mult, op1=ALU.mult),
                        rd=["ee", ("l_qs", p)], wr=[("qt", di)])
                for c in range(NCH):
                    bank, piece = (c // 8) % 2, c % 8
                    pv = psb[bank][0:64, piece * 128:(piece + 1) * 128]
                    S.add("pe", lambda e, pv=pv, c=c, di=di: e.transpose(out=pv, in_=kt[di][:, c * 64:(c + 1) * 64],
                                                                        identity=ident),
                          rd=[("kt", di), "ident"], wr=[("ps", bank)])
                    if piece == 7 or c == NCH - 1:
                        n0 = c - piece
                        S.add("act", lambda e, bank=bank, n0=n0, piece=piece: e.copy(
                            out=ktok[:, n0 * 128:(n0 + piece + 1) * 128], in_=psb[bank][0:64, 0:(piece + 1) * 128]),
                            rd=[("ps", bank)], wr=["ktok"])
                p3s = int(os.environ.get("P3STOP", "100"))
                if full and p3s >= 2:
                    for c in range(NCH):
                        bank, piece = 2 + (c // CPB) % 2, c % CPB
                        S.add("pe", lambda e, bank=bank, piece=piece, c=c, di=di: e.matmul(
                            ps[bank][0:64, piece * 64:(piece + 1) * 64], lhsT=kt[di][:, c * 64:(c + 1) * 64],
                            rhs=qt[di][:, c * 64:(c + 1) * 64], start=True, stop=True),
                            rd=[("kt", di), ("qt", di)], wr=[("ps", bank)])
                        if piece == CPB - 1:
                            n0 = c - piece
                            S.add("dve", lambda e, bank=bank, n0=n0, di=di, d=d: e.tensor_tensor(
                                out=scT[di][:, n0 * 64:(n0 + CPB) * 64], in0=ps[bank][0:64, 0:CPB * 64],
                                in1=cm[:, d * CPB * 64:(d + 1) * CPB * 64], op=ALU.mult),
                                rd=[("ps", bank), "cm"], wr=[("scT", di)])
                order = list(range(NCH - 1, -1, -1)) if bwd else list(range(NCH))
                if full:
                    stin = SB if bwd else SF
                    stkey = "SB" if bwd else "SF"
                    if blk == 0:
                        S.add("dve", lambda e: e.memset(Tst, 0.0), wr=["Tst"])
                    else:
                        S.add("dve", lambda e, stin=stin, h=h: e.tensor_copy(out=Tst, in_=stin[:, h * 128:(h + 1) * 128]),
                              rd=[stkey], wr=["Tst"])
                else:
                    s_ = ctx_slot
                    S.add("dve", lambda e, h=h, s_=s_: e.tensor_scalar(out=Tst, in0=SF[:, h * 128:(h + 1) * 128],
                                                                      scalar1=mk[:, s_:s_ + 1], scalar2=None,
                                                                      op0=ALU.mult), rd=["SF", "mk"], wr=["Tst"])
                    S.add("dve", lambda e, h=h, s_=s_: e.scalar_tensor_tensor(
                        out=Tst, in0=SB[:, h * 128:(h + 1) * 128], scalar=mk[:, 3 + s_:4 + s_], in1=Tst,
                        op0=ALU.mult, op1=ALU.add), rd=["SB", "mk", "Tst"], wr=["Tst"])
                for oi, c in enumerate(order):
                    bank = 4 + oi % 2
                    pa = ps[bank][:, 0:128]
                    pk = ("ps", bank)
                    S.add("pe", lambda e, pa=pa, c=c: e.matmul(pa, lhsT=ktok[:, c * 128:(c + 1) * 128],
                                                               rhs=vtok[:, c * 128:(c + 1) * 128], start=True, stop=True),
                          rd=["ktok", "vtok"], wr=[pk])
                    if bwd:
                        sc = dcol[:, c:c + 1]
                    else:
                        sc = dcol[:, NCH:NCH + 1] if c == 0 else dcol[:, c - 1:c]
                    if full:
                        S.add("dve", lambda e, c=c, sc=sc, di=di: e.tensor_scalar(
                            out=Xall[di][:, c * 128:(c + 1) * 128], in0=Tst, scalar1=sc, scalar2=None, op0=ALU.mult),
                            rd=["Tst", "dcol", "dcol1"], wr=[("Xall", di)])
                    S.add("dve", lambda e, sc=sc, pa=pa: e.scalar_tensor_tensor(
                        out=Tst, in0=Tst, scalar=sc, in1=pa, op0=ALU.mult, op1=ALU.add),
                        rd=["Tst", "dcol", "dcol1", pk], wr=["Tst"])
                if not full:
                    s_ = ctx_slot
                    S.add("dve", lambda e: e.tensor_scalar(out=Tst, in0=Tst, scalar1=dcol[:, NCH - 1:NCH], scalar2=None,
                                                           op0=ALU.mult), rd=["Tst", "dcol"], wr=["Tst"])
                    for (stt, key, mcol) in ((SF, "SF", s_), (SB, "SB", 3 + s_)):
                        sv = stt[:, h * 128:(h + 1) * 128]
                        S.add("dve", lambda e, sv=sv: e.tensor_tensor(out=tmpS, in0=Tst, in1=sv, op=ALU.subtract),
                              rd=["Tst", key], wr=["tmpS"])
                        S.add("dve", lambda e, sv=sv, mcol=mcol: e.scalar_tensor_tensor(
                            out=sv, in0=tmpS, scalar=mk[:, mcol:mcol + 1], in1=sv, op0=ALU.mult, op1=ALU.add),
                            rd=["tmpS", "mk", key], wr=[key])
            if not full or p3s < 4:
                continue
            ao = ast[h % 2]
            aok = ("ast", h % 2)
            ga = bufs["ga"][p]
            for n in range(NS):
                bank = 6
                for j in range(CPB):
                    c = n * CPB + j
                    po = ps[bank][:, j * 64:(j + 1) * 64]
                    for di in range(2):
                        S.add("pe", lambda e, po=po, c=c, di=di: e.matmul(
                            po, lhsT=vtok[:, c * 128:(c + 1) * 128], rhs=scT[di][:, c * 64:(c + 1) * 64],
                            start=(di == 0), stop=False), rd=["vtok", ("scT", di)], wr=[("ps", bank)])
                        S.add("pe", lambda e, po=po, c=c, di=di: e.matmul(
                            po, lhsT=Xall[di][:, c * 128:(c + 1) * 128], rhs=qt[di][:, c * 64:(c + 1) * 64],
                            start=False, stop=(di == 1)), rd=[("Xall", di), ("qt", di)], wr=[("ps", bank)])
                if p3s < 5:
                    continue
                S.add("act", lambda e: e.activation(out=osq, in_=ps[6][:, 0:NSUB], func=AF.Square),
                      rd=[("ps", 6)], wr=["osq"])
                S.add("dve", lambda e: e.tensor_copy(out=osb, in_=ps[6][:, 0:NSUB]), rd=[("ps", 6)], wr=["osb"])
                S.add("pe", lambda e: e.matmul(ps[7][:, 0:NSUB], lhsT=ones, rhs=osq, start=True, stop=True),
                      rd=["ones", "osq"], wr=[("ps", 7)])
                S.add("act", lambda e: e.activation(out=rst, in_=ps[7][:, 0:NSUB], func=AF.Sqrt, bias=epsc[:, 0:1]),
                      rd=[("ps", 7), "epsc"], wr=["rst"])
                S.add("dve", lambda e: e.reciprocal(out=rst, in_=rst), rd=["rst"], wr=["rst"])
                S.add("dve", lambda e, h=h: e.scalar_tensor_tensor(out=osb, in0=osb, scalar=vcol(1, h), in1=rst,
                                                                   op0=ALU.mult, op1=ALU.mult),
                      rd=["osb", "rst", "vec"], wr=["osb"])
                S.add("dve", lambda e, ao=ao, ga=ga, n=n: e.tensor_tensor(
                    out=ao[:, n * NSUB:(n + 1) * NSUB], in0=osb, in1=ga[:, n * NSUB:(n + 1) * NSUB], op=ALU.mult),
                    rd=["osb", ("l_ga", p)], wr=[aok])
            dma("pool", s_ao[blk, h * 128:(h + 1) * 128, :], ao, [aok], [("s_ao", blk)])
        if not full:
            dma("pool", s_st[0], SF, ["SF"], ["s_st"])
            dma("pool", s_st[1], SB, ["SB"], ["s_st"])
        A.release(lm)

    def p4a(blk):
        lm = A.mark()
        uT = A.alloc(KB * T, BF16)
        uT3 = uT.rearrange("p (k t) -> p k t", t=T)
        tc_ = A.alloc(2 * KG * CG, BF16)
        tc4 = tc_.rearrange("p (a k c) -> p a k c", a=2, k=KG)
        dma("sp", uT3, s_u[blk].rearrange("(k p) t -> p k t", p=128), [("s_u", blk, j) for j in range(KB)], ["uT"])
        dma("sp", tc4, tcc.rearrange("a (k p) c -> p a k c", p=128), [], ["tcc"])
        sto = [A.alloc(2 * DB, BF16) for _ in range(2)]
        for i in range(NT):
            so = sto[i % 2]
            sk = ("sto", i % 2)
            for g4 in range(4):
                for cs in range(2):
                    bank = 2 + cnt["ps"] % 4
                    cnt["ps"] += 1
                    pv = ps[bank][:, 0:CG]
                    for k in range(KG):
                        S.add("pe", lambda e, pv=pv, g4=g4, k=k, cs=cs, i=i: e.matmul(
                            pv, lhsT=uT3[:, g4 * KG + k, i * 128:(i + 1) * 128], rhs=tc4[:, cs, k, :],
                            start=(k == 0), stop=(k == KG - 1)), rd=["uT", "tcc"], wr=[("ps", bank)])
                    o = so[:, cs * DB + g4 * CG:cs * DB + (g4 + 1) * CG]
                    if cs == 0:
                        S.add("act", lambda e, pv=pv, o=o: e.copy(out=o, in_=pv), rd=[("ps", bank)], wr=[sk])
                    else:
                        S.add("dve", lambda e, pv=pv, o=o: e.tensor_copy(out=o, in_=pv), rd=[("ps", bank)], wr=[sk])
            for cs in range(2):
                dma("pool", s_uc[blk, cs, i * 128:(i + 1) * 128, :], so[:, cs * DB:(cs + 1) * DB], [sk],
                    [("s_uc", blk)])
        A.release(lm)

    def p4b(blk, slots):
        lm = A.mark()
        GW = min(256, DB)
        NG = DB // GW
        nsl = len(slots)
        ucs = A.alloc(nsl * 2 * NT * GW, BF16)
        uc5 = ucs.rearrange("p (s a i c) -> p s a i c", s=nsl, a=2, i=NT)
        tb = [A.alloc(2 * NT * NSUB, BF16) for _ in range(2)]
        gbt = [A.alloc(NSUB, BF16) for _ in range(2)]
        bst = [A.alloc(NSUB, BF16) for _ in range(2)]
        nt = 0
        for gq in range(NG):
            for si, (sb, ti) in enumerate(slots):
                for a in range(2):
                    dma("sp", uc5[:, si, a, :, :],
                        s_uc[sb, a, :, gq * GW:(gq + 1) * GW].rearrange("(i p) c -> p i c", p=128),
                        [("s_uc", sb)], ["ucs"])
            for n in range(NS):
                nblk = GW // 128
                for si, (sb, ti) in enumerate(slots):
                    tbi = nt % 2
                    nt += 1
                    t4 = tb[tbi].rearrange("p (a i l) -> p a i l", a=2, i=NT)
                    for a in range(2):
                        dma("sp", t4[:, a, :, :],
                            tseq[ti, a, :, n * NSUB:(n + 1) * NSUB].rearrange("(i p) l -> p i l", p=128),
                            [], [("tb", tbi)])
                    for cb in range(nblk):
                        pv = ps[2 + cb][:, 0:NSUB]
                        for a in range(2):
                            for i in range(NT):
                                first = (si == 0 and a == 0 and i == 0)
                                last = (si == nsl - 1 and a == 1 and i == NT - 1)
                                S.add("pe", lambda e, pv=pv, si=si, a=a, i=i, cb=cb, t4=t4, first=first, last=last: e.matmul(
                                    pv, lhsT=uc5[:, si, a, i, cb * 128:(cb + 1) * 128], rhs=t4[:, a, i, :],
                                    start=first, stop=last), rd=["ucs", ("tb", tbi)], wr=[("ps", 2 + cb)])
                for cb in range(nblk):
                    ch = gq * nblk + cb
                    q2 = cnt["ev"] % 2
                    cnt["ev"] += 1
                    dma("sp", gbt[q2], s_gb[blk, ch * 128:(ch + 1) * 128, n * NSUB:(n + 1) * NSUB],
                        [("s_gb", blk, ch)], [("gbt", q2)])
                    S.add("dve", lambda e, cb=cb, q2=q2: e.tensor_tensor(out=bst[q2], in0=ps[2 + cb][:, 0:NSUB], in1=gbt[q2],
                                                                        op=ALU.mult),
                          rd=[("ps", 2 + cb), ("gbt", q2)], wr=[("bst", q2)])
                    dma("pool", s_bo[blk, ch * 128:(ch + 1) * 128, n * NSUB:(n + 1) * NSUB], bst[q2],
                        [("bst", q2)], [("s_bo", blk)])
        A.release(lm)

    def p5a(blk):
        lm = A.mark()
        aT = A.alloc(KC * NSUB, BF16)
        aT3 = aT.rearrange("p (k t) -> p k t", t=NSUB)
        bT = A.alloc(KB * NSUB, BF16)
        bT3 = bT.rearrange("p (k t) -> p k t", t=NSUB)
        was = [A.alloc(KC * 256, BF16) for _ in range(2)]
        wbs = [A.alloc(KB * 256, BF16) for _ in range(2)]
        gm = [[A.alloc(NSUB, BF16) for _ in range(2)] for _ in range(2)]
        t1 = A.alloc(NSUB, F32)
        t2 = A.alloc(NSUB, F32)
        mst = [A.alloc(NSUB, BF16) for _ in range(2)]
        nw = 0
        for n in range(NS):
            tsl = slice(n * NSUB, (n + 1) * NSUB)
            dma("sp", aT3, s_ao[blk, :, tsl].rearrange("(k p) t -> p k t", p=128), [("s_ao", blk)], ["aT"])
            dma("sp", bT3, s_bo[blk, :, tsl].rearrange("(k p) t -> p k t", p=128), [("s_bo", blk)], ["bT"])
            for dg in range(0, KC, 2):
                wi = nw % 2
                nw += 1
                wa3 = was[wi].rearrange("p (k c) -> p k c", c=256)
                wb3 = wbs[wi].rearrange("p (k c) -> p k c", c=256)
                dma("sp", wa3, wab[:, dg * 128:dg * 128 + 256].rearrange("(k p) c -> p k c", p=128), ["wab"], [("wa", wi)])
                dma("sp", wb3, wbb[:, dg * 128:dg * 128 + 256].rearrange("(k p) c -> p k c", p=128), ["wbb"], [("wb", wi)])
                for gi in range(2):
                    dblk = dg + gi
                    q2 = cnt["ev"] % 2
                    cnt["ev"] += 1
                    dma("sp", gm[0][q2], s_ma[blk, dblk * 128:(dblk + 1) * 128, tsl], [("s_ma", blk, dblk)], [("gma", q2)])
                    dma("sp", gm[1][q2], s_mb[blk, dblk * 128:(dblk + 1) * 128, tsl], [("s_mb", blk, dblk)], [("gmb", q2)])
                    pa, pb = ps[2 + (2 * cnt["ev"]) % 4], ps[2 + (2 * cnt["ev"]) % 4 + 1]
                    ka, kb = ("ps", 2 + (2 * cnt["ev"]) % 4), ("ps", 2 + (2 * cnt["ev"]) % 4 + 1)
                    for k in range(KC):
                        S.add("pe", lambda e, pa=pa, wa3=wa3, k=k, gi=gi: e.matmul(
                            pa[:, 0:NSUB], lhsT=wa3[:, k, gi * 128:(gi + 1) * 128], rhs=aT3[:, k, :],
                            start=(k == 0), stop=(k == KC - 1)), rd=[("wa", wi), "aT"], wr=[ka])
                    for k in range(KB):
                        S.add("pe", lambda e, pb=pb, wb3=wb3, k=k, gi=gi: e.matmul(
                            pb[:, 0:NSUB], lhsT=wb3[:, k, gi * 128:(gi + 1) * 128], rhs=bT3[:, k, :],
                            start=(k == 0), stop=(k == KB - 1)), rd=[("wb", wi), "bT"], wr=[kb])
                    S.add("dve", lambda e, pa=pa, q2=q2: e.tensor_tensor(out=t1, in0=pa[:, 0:NSUB], in1=gm[0][q2], op=ALU.mult),
                          rd=[ka, ("gma", q2)], wr=["t1"])
                    S.add("dve", lambda e, pb=pb, q2=q2: e.tensor_tensor(out=t2, in0=pb[:, 0:NSUB], in1=gm[1][q2], op=ALU.mult),
                          rd=[kb, ("gmb", q2)], wr=["t2"])
                    S.add("dve", lambda e, q2=q2: e.tensor_tensor(out=mst[q2], in0=t1, in1=t2, op=ALU.add),
                          rd=["t1", "t2"], wr=[("mst", q2)])
                    dma("pool", s_m[blk, dblk * 128:(dblk + 1) * 128, tsl], mst[q2], [("mst", q2)], [("s_m", blk)])
        A.release(lm)

    def p5b(blk):
        lm = A.mark()
        EW = min(512, D)
        NE = D // EW
        mT = A.alloc(KC * NSUB, BF16)
        mT3 = mT.rearrange("p (k t) -> p k t", t=NSUB)
        wos = [A.alloc(KC * EW, BF16) for _ in range(2)]
        ntl = NSUB // 128
        yp = A.alloc(ntl * D, F32)
        yp3 = yp.rearrange("p (i d) -> p i d", d=D)
        gf = A.alloc(D, F32)
        junk = A.alloc(D, BF16)
        ss = A.alloc(2, F32)
        dma("sp", gf, gfin, [], ["gf"])
        nw = 0
        for n in range(NS):
            tsl = slice(n * NSUB, (n + 1) * NSUB)
            dma("sp", mT3, s_m[blk, :, tsl].rearrange("(k p) t -> p k t", p=128), [("s_m", blk)], ["mT"])
            for i in range(ntl):
                dma("sp", yp3[:, i, :], xb[blk, n * NSUB + i * 128:n * NSUB + (i + 1) * 128, :], [], [("yp", i)])
            for eb in range(NE):
                wi = nw % 2
                nw += 1
                wo3 = wos[wi].rearrange("p (k c) -> p k c", c=EW)
                dma("sp", wo3, wob[:, eb * EW:(eb + 1) * EW].rearrange("(k p) c -> p k c", p=128), ["wob"], [("wo", wi)])
                for i in range(ntl):
                    bank = 2 + cnt["ps"] % 4
                    cnt["ps"] += 1
                    pv = ps[bank][:, 0:EW]
                    for k in range(KC):
                        S.add("pe", lambda e, pv=pv, wo3=wo3, k=k, i=i: e.matmul(
                            pv, lhsT=mT3[:, k, i * 128:(i + 1) * 128], rhs=wo3[:, k, :],
                            start=(k == 0), stop=(k == KC - 1)), rd=["mT", ("wo", wi)], wr=[("ps", bank)])
                    S.add("dve", lambda e, pv=pv, i=i, eb=eb: e.tensor_tensor(
                        out=yp3[:, i, eb * EW:(eb + 1) * EW], in0=pv, in1=yp3[:, i, eb * EW:(eb + 1) * EW], op=ALU.add),
                        rd=[("ps", bank), ("yp", i)], wr=[("yp", i)])
            for i in range(ntl):
                S.add("dve", lambda e: e.memset(ss, 0.0), wr=["ss5"])
                S.add("act", lambda e, i=i: e.activation(out=junk, in_=yp3[:, i, :], func=AF.Square, scale=float(D) ** -0.5, accum_out=ss[:, 0:1]),
                      rd=[("yp", i), "ss5"], wr=["junk5", "ss5"])
                S.add("act", lambda e: e.activation(out=ss[:, 1:2], in_=ss[:, 0:1], func=AF.Sqrt, bias=epsc[:, 0:1]),
                      rd=["ss5", "epsc"], wr=["ss51"])
                S.add("dve", lambda e: e.reciprocal(out=ss[:, 1:2], in_=ss[:, 1:2]), rd=["ss51"], wr=["ss51"])
                S.add("dve", lambda e, i=i: e.scalar_tensor_tensor(out=yp3[:, i, :], in0=yp3[:, i, :], scalar=ss[:, 1:2],
                                                                   in1=gf, op0=ALU.mult, op1=ALU.mult),
                      rd=[("yp", i), "ss51", "gf"], wr=[("yp", i)])
                r0 = n * NSUB + i * 128
                dma("pool", y[blk, r0:r0 + 128, :], yp3[:, i, :], [("yp", i)], [("y", blk)])
        A.release(lm)

    import os
    kstop = int(os.environ.get("KSTOP", "1000"))
    nph = [0]

    def ph(f, *a):
        nph[0] += 1
        if nph[0] > kstop:
            return None
        r = f(*a)
        S.barrier()
        return r

    def run_block(blk, ctx_slot):
        m = A.mark()
        hT = A.alloc(KC * T, BF16)
        m1 = ph(p1, blk, hT)
        A.release(m1 if m1 is not None else A.mark())
        ph(p2, blk, hT, ctx_slot)
        A.release(m)
        ph(p3, blk, ctx_slot)
        ph(p4a, blk)

    for s in range(3):
        run_block(2 + s, s)
    run_block(1, None)
    run_block(0, None)
    ph(p4b, 0, [(0, 0)])
    ph(p4b, 1, [(1, 1), (2, 2), (3, 3), (4, 4)])
    for blk in (0, 1):
        ph(p5a, blk)
        ph(p5b, blk)

    S.emit(nc, stack)
    stack.close()
    return nc


def _bf16(a):
    return np.asarray(a, np.float32).astype(ml_dtypes.bfloat16)


def make_in_maps(cfg, x_prompt, x_sample, norm_gain, w_in, lower_bounds_fwd, lower_bounds_bwd,
                 head_norm_gain, w_branch_a, w_branch_b, w_out, final_norm_gain):
    D, T, KC, CG, DB = cfg.D, cfg.T, cfg.KC, cfg.CG, cfg.DB
    L = 4 * T
    f32 = np.float32
    w_in0 = np.ascontiguousarray(np.asarray(w_in, f32)[0])
    wa = np.ascontiguousarray(np.asarray(w_branch_a, f32)[0])
    wb = np.ascontiguousarray(np.asarray(w_branch_b, f32)[0])
    wo = np.ascontiguousarray(np.asarray(w_out, f32)[0])
    zf = [np.ascontiguousarray(w_in0[:, D:2 * D]), np.ascontiguousarray(w_in0[:, 2 * D:3 * D])]
    lbs = [np.asarray(lower_bounds_fwd, f32), np.asarray(lower_bounds_bwd, f32)]
    xp = np.asarray(x_prompt, f32)
    xs = np.asarray(x_sample, f32)

    def cols(v):
        return np.asarray(v, f32).reshape(KC, 128).T

    cc = np.arange(CG)
    ang = 2 * np.pi * np.outer(cc, cc) / CG
    tcc = _bf16(np.stack([np.cos(ang), np.sin(ang)]))
    CPB = cfg.CPB
    tri = (np.arange(64)[:, None] <= np.arange(64)[None, :]).astype(f32)
    cmask = _bf16(np.stack([np.tile(tri, (1, CPB)), np.tile(tri.T, (1, CPB))], axis=1))
    m01 = np.ones((128, T), f32)
    m01[:, ::CH] = 0.0
    ident = _bf16(np.eye(128))
    gfin = np.ascontiguousarray(np.broadcast_to(np.asarray(final_norm_gain, f32)[None, :], (128, D)))
    gnb = np.ascontiguousarray(np.broadcast_to(np.asarray(norm_gain, f32)[0][None, :], (128, D)))

    def seq_table(pos_src, pos_dst, Ltot):
        a = 2 * np.pi * ((pos_src[:, None].astype(np.int64) * pos_dst[None, :].astype(np.int64)) % Ltot) / Ltot
        sc = (CG * Ltot) ** -0.5
        return np.stack([np.cos(a) * sc, -np.sin(a) * sc])

    tab_sample = seq_table(np.arange(T), np.arange(T), T)
    in_maps = []
    for c in range(8):
        p, j = c // 4, c % 4
        blocks = [xs[c], xp[p, j * T:(j + 1) * T]]
        pos = [None, np.arange(j * T, (j + 1) * T)]
        ctx = [(b, 0) for b in range(0, j)] + [(b, 1) for b in range(3, j, -1)]
        wctx, vl, mF, mB = [], [], [], []
        for (b, d) in ctx:
            xblk = xp[p, b * T:(b + 1) * T]
            pp = np.arange(b * T, (b + 1) * T)
            if d == 1:
                xblk, pp = xblk[::-1], pp[::-1]
            blocks.append(xblk)
            pos.append(pp)
            wctx.append(zf[d])
            vl.append(lbs[d])
            mF.append(1.0 if d == 0 else 0.0)
            mB.append(1.0 if d == 1 else 0.0)
        vecs = np.zeros((128, 16 * KC), f32)
        vecs[:, 0:KC] = cols(np.asarray(norm_gain, f32)[0])
        vecs[:, KC:2 * KC] = cols(np.asarray(head_norm_gain, f32)[0].reshape(-1))
        for si, lbv in enumerate([lbs[0], lbs[1]] + vl):
            vecs[:, (2 + 2 * si) * KC:(3 + 2 * si) * KC] = cols(lbv[0])
            vecs[:, (3 + 2 * si) * KC:(4 + 2 * si) * KC] = cols(lbv[1])
        msk = np.zeros((128, 8), f32)
        msk[:, 0:3] = np.asarray(mF, f32)[None, :]
        msk[:, 3:6] = np.asarray(mB, f32)[None, :]
        msk[:, 6] = 1.0
        tabs = [tab_sample] + [seq_table(pos[s], pos[1], L) for s in range(1, 5)]
        in_maps.append({
            "xb": np.ascontiguousarray(np.stack(blocks)),
            "w_in": w_in0, "w_ctx": np.ascontiguousarray(np.stack(wctx)), "w_a": wa, "w_b": wb, "w_o": wo,
            "vecs": vecs, "gfin": gfin, "gnb": gnb, "msk": msk, "tcc": tcc,
            "tseq": _bf16(np.stack(tabs)), "cmask": cmask, "m01": m01, "identd": ident,
        })
    return in_maps


def run(cfg, **inputs):
    in_maps = make_in_maps(cfg, **inputs)
    nc = build(cfg)
    res = run_bass_kernel_spmd(nc, in_maps, core_ids=list(range(8)))
    T = cfg.T
    B = inputs["x_prompt"].shape[0]
    y_prompt = np.zeros(inputs["x_prompt"].shape, np.float32)
    y_sample = np.zeros(inputs["x_sample"].shape, np.float32)
    for c in range(8):
        yc = res.results[c]["y"]
        y_sample[c] = yc[0]
        y_prompt[c // 4, (c % 4) * T:(c % 4 + 1) * T] = yc[1]
    return y_prompt, y_sample


def kernel(**inputs):
    cfg = Cfg(4096, 2048)
    inputs = {k: np.asarray(v) for k, v in inputs.items()}
    return run(cfg, **inputs)
```
